# Optimizing a Trainium2 kernel written in Bass

```python
import math
import jax, jax.numpy as jnp
from jax import lax
import numpy as np

D_MODEL = 1024
BATCH = 4
SEQ = 8192
DEPTH = 2

CHUNK = 64
N_A_LAYERS = DEPTH // 2
N_B_LAYERS = DEPTH - N_A_LAYERS

RWKV_HEAD = 64
RWKV_HEADS = D_MODEL // RWKV_HEAD
DECAY_LORA = 64
A_LORA = 64
GATE_LORA = 160
LNX_EPS = 64e-5

DA_HEAD_DIM = 64
DA_HEADS = D_MODEL // (2 * DA_HEAD_DIM)
DA_V_DIM = 2 * DA_HEAD_DIM
ROPE_DIM = DA_HEAD_DIM // 4
ROPE_THETA = 500000.0
Q_BLOCK = 128
SUBLN_EPS = 1e-5

D_FF = ((8 * D_MODEL + 3 * 256 - 1) // (3 * 256)) * 256

NORM_EPS = 1e-6

kernel_name = "yoco_rwkv7_diffattn_trunk"


def rms_norm(x, g, eps=NORM_EPS):
    xf = x.astype(jnp.float32)
    y = xf * lax.rsqrt(jnp.mean(xf * xf, axis=-1, keepdims=True) + eps)
    return (y * g.astype(jnp.float32)).astype(x.dtype)


def swiglu_ffn(h, w_gu, w_down):
    gate, up = jnp.split(h @ w_gu, 2, axis=-1)
    return (jax.nn.silu(gate) * up) @ w_down


def partial_rope(x, pos):
    half = ROPE_DIM // 2
    inv = jnp.power(ROPE_THETA, -jnp.arange(0, ROPE_DIM, 2, dtype=jnp.float32) / ROPE_DIM)
    ang = pos[:, None] * inv[None, :]
    cos = jnp.cos(ang)[None, :, None, :]
    sin = jnp.sin(ang)[None, :, None, :]
    xf = x.astype(jnp.float32)
    x1 = xf[..., :half]
    x2 = xf[..., half:ROPE_DIM]
    out = jnp.concatenate([x1 * cos - x2 * sin, x2 * cos + x1 * sin, xf[..., ROPE_DIM:]], axis=-1)
    return out.astype(x.dtype)


def rwkv7_time_mix(h, mu, w_r, w_k, w_v, w_o, w0, w1, w2, a0, a1, a2, g1, g2,
                   k_k, k_a, r_k, lnx_w, lnx_b):
    B, S, D = h.shape
    H, N = RWKV_HEADS, RWKV_HEAD
    h_prev = jnp.pad(h, ((0, 0), (1, 0), (0, 0)))[:, :-1]
    hh = h_prev - h
    xr = h + hh * mu[0]
    xw = h + hh * mu[1]
    xk = h + hh * mu[2]
    xv = h + hh * mu[3]
    xa = h + hh * mu[4]
    xg = h + hh * mu[5]

    r = xr @ w_r
    w = -jax.nn.softplus(-(w0 + jnp.tanh(xw @ w1) @ w2)) - 0.5
    k = xk @ w_k
    v = xv @ w_v
    a = jax.nn.sigmoid(a0 + (xa @ a1) @ a2)
    g = jax.nn.sigmoid(xg @ g1) @ g2

    kk = (k * k_k).astype(jnp.float32).reshape(B, S, H, N)
    kk = kk / jnp.maximum(jnp.sqrt(jnp.sum(kk * kk, axis=-1, keepdims=True)), 1e-12)
    k = k * (1.0 + (a - 1.0) * k_a)

    def heads(t):
        return t.astype(jnp.float32).reshape(B, S, H, N)

    r_h, k_h, v_h, a_h = heads(r), heads(k), heads(v), heads(a)
    decay = jnp.exp(-jnp.exp(heads(w)))
    a_vec = -kk
    b_vec = kk * a_h

    def time_major(t):
        return t.transpose(1, 0, 2, 3)

    def step(state, inp):
        r_t, w_t, k_t, v_t, a_t, b_t = inp
        sa = jnp.einsum('bhvk,bhk->bhv', state, a_t)
        state = (state * w_t[:, :, None, :]
                 + sa[..., None] * b_t[:, :, None, :]
                 + v_t[..., None] * k_t[:, :, None, :])
        y_t = jnp.einsum('bhvk,bhk->bhv', state, r_t)
        return state, y_t

    state0 = jnp.zeros((B, H, N, N), jnp.float32)
    _, y = lax.scan(step, state0, (time_major(r_h), time_major(decay), time_major(k_h),
                                   time_major(v_h), time_major(a_vec), time_major(b_vec)))
    y = y.transpose(1, 0, 2, 3)

    mean = jnp.mean(y, axis=-1, keepdims=True)
    var = jnp.mean(jnp.square(y - mean), axis=-1, keepdims=True)
    y = ((y - mean) * lax.rsqrt(var + LNX_EPS)).reshape(B, S, D)
    y = y * lnx_w.astype(jnp.float32) + lnx_b.astype(jnp.float32)
    bonus = jnp.sum(r_h * k_h * r_k.astype(jnp.float32), axis=-1, keepdims=True) * v_h
    y = y + bonus.reshape(B, S, D)
    return (y * g.astype(jnp.float32)).astype(h.dtype) @ w_o


def shared_kv(x, kv_g, kv_w, k_norm_g, pos):
    B, S, _ = x.shape
    hkv = rms_norm(x, kv_g)
    k, v = jnp.split(hkv @ kv_w, 2, axis=-1)
    k = k.reshape(B, S, 2 * DA_HEADS, DA_HEAD_DIM)
    k = partial_rope(rms_norm(k, k_norm_g), pos)
    v = v.reshape(B, S, DA_HEADS, DA_V_DIM)
    return k, v


def diff_attention(h, k, v, pos, w_q, q_norm_g, lam_q1, lam_k1, lam_q2, lam_k2,
                   subln_g, w_o, lam_init):
    B, S, D = h.shape
    H = DA_HEADS
    q = (h @ w_q).reshape(B, S, 2 * H, DA_HEAD_DIM)
    q = partial_rope(rms_norm(q, q_norm_g), pos)
    f32 = jnp.float32
    lam = (jnp.exp(jnp.sum(lam_q1.astype(f32) * lam_k1.astype(f32)))
           - jnp.exp(jnp.sum(lam_q2.astype(f32) * lam_k2.astype(f32))) + lam_init)
    scale = DA_HEAD_DIM ** -0.5
    k_chunk = jnp.arange(S) // CHUNK
    n_blocks = S // Q_BLOCK

    def block(i):
        start = i * Q_BLOCK
        q_blk = lax.dynamic_slice_in_dim(q, start, Q_BLOCK, axis=1)
        s = jnp.einsum('bqhd,bkhd->bhqk', q_blk, k).astype(f32) * scale
        q_chunk = (start + jnp.arange(Q_BLOCK)) // CHUNK
        allowed = k_chunk[None, :] <= q_chunk[:, None]
        s = jnp.where(allowed[None, None], s, -jnp.inf)
        p = jax.nn.softmax(s, axis=-1).reshape(B, H, 2, Q_BLOCK, S)
        p = p[:, :, 0] - lam * p[:, :, 1]
        return jnp.einsum('bhqk,bkhe->bqhe', p.astype(v.dtype), v)

    o = lax.map(block, jnp.arange(n_blocks))
    o = o.transpose(1, 0, 2, 3, 4).reshape(B, S, H, DA_V_DIM)
    o = rms_norm(o, subln_g, eps=SUBLN_EPS) * (1.0 - lam_init)
    return o.reshape(B, S, D) @ w_o


def setup_inputs(seed: int = 0) -> dict:
    key = jax.random.key(seed)
    ks = iter(jax.random.split(key, 48))
    D, F, NA, NB = D_MODEL, D_FF, N_A_LAYERS, N_B_LAYERS
    nrm = lambda shape, s: jax.random.normal(next(ks), shape, jnp.float32) * s
    gain = lambda shape: 1.0 + 0.02 * jax.random.normal(next(ks), shape, jnp.float32)
    inp = {}
    inp["x"] = nrm((BATCH, SEQ, D), 1.0)
    inp["g_mix"] = gain((DEPTH, D))
    inp["g_ffn"] = gain((DEPTH, D))
    inp["rw_mu"] = jax.random.uniform(next(ks), (NA, 6, D), jnp.float32)
    inp["rw_w_r"] = nrm((NA, D, D), D ** -0.5)
    inp["rw_w_k"] = nrm((NA, D, D), D ** -0.5)
    inp["rw_w_v"] = nrm((NA, D, D), D ** -0.5)
    inp["rw_w_o"] = nrm((NA, D, D), D ** -0.5)
    inp["rw_w0"] = jax.random.uniform(next(ks), (NA, D), jnp.float32, -6.0, -1.0)
    inp["rw_w1"] = nrm((NA, D, DECAY_LORA), D ** -0.5)
    inp["rw_w2"] = nrm((NA, DECAY_LORA, D), 0.5 * DECAY_LORA ** -0.5)
    inp["rw_a0"] = nrm((NA, D), 0.1)
    inp["rw_a1"] = nrm((NA, D, A_LORA), D ** -0.5)
    inp["rw_a2"] = nrm((NA, A_LORA, D), A_LORA ** -0.5)
    inp["rw_g1"] = nrm((NA, D, GATE_LORA), D ** -0.5)
    inp["rw_g2"] = nrm((NA, GATE_LORA, D), GATE_LORA ** -0.5)
    inp["rw_k_k"] = 0.85 + nrm((NA, D), 0.05)
    inp["rw_k_a"] = 1.0 + nrm((NA, D), 0.05)
    inp["rw_r_k"] = nrm((NA, RWKV_HEADS, RWKV_HEAD), 0.1)
    inp["rw_lnx_w"] = gain((NA, D))
    inp["rw_lnx_b"] = nrm((NA, D), 0.02)
    inp["kv_g"] = gain((D,))
    inp["kv_w"] = nrm((D, 2 * D), D ** -0.5)
    inp["k_norm_g"] = gain((DA_HEAD_DIM,))
    inp["da_w_q"] = nrm((NB, D, D), D ** -0.5)
    inp["da_q_norm_g"] = gain((NB, DA_HEAD_DIM))
    inp["da_lam_q1"] = nrm((NB, DA_HEAD_DIM), 0.1)
    inp["da_lam_k1"] = nrm((NB, DA_HEAD_DIM), 0.1)
    inp["da_lam_q2"] = nrm((NB, DA_HEAD_DIM), 0.1)
    inp["da_lam_k2"] = nrm((NB, DA_HEAD_DIM), 0.1)
    inp["da_subln_g"] = gain((NB, DA_V_DIM))
    inp["da_w_o"] = nrm((NB, D, D), D ** -0.5)
    inp["ffn_w_gu"] = nrm((DEPTH, D, 2 * F), D ** -0.5)
    inp["ffn_w_down"] = nrm((DEPTH, F, D), F ** -0.5)
    return inp


def reference(x, g_mix, g_ffn, rw_mu, rw_w_r, rw_w_k, rw_w_v, rw_w_o, rw_w0, rw_w1, rw_w2,
              rw_a0, rw_a1, rw_a2, rw_g1, rw_g2, rw_k_k, rw_k_a, rw_r_k, rw_lnx_w, rw_lnx_b,
              kv_g, kv_w, k_norm_g, da_w_q, da_q_norm_g, da_lam_q1, da_lam_k1, da_lam_q2,
              da_lam_k2, da_subln_g, da_w_o, ffn_w_gu, ffn_w_down):
    S = x.shape[1]
    pos = jnp.arange(S, dtype=jnp.float32)
    k_sh, v_sh = None, None
    for layer in range(DEPTH):
        if layer < N_A_LAYERS:
            i = layer
            h = rms_norm(x, g_mix[layer])
            x = x + rwkv7_time_mix(h, rw_mu[i], rw_w_r[i], rw_w_k[i], rw_w_v[i], rw_w_o[i],
                                   rw_w0[i], rw_w1[i], rw_w2[i], rw_a0[i], rw_a1[i], rw_a2[i],
                                   rw_g1[i], rw_g2[i], rw_k_k[i], rw_k_a[i], rw_r_k[i],
                                   rw_lnx_w[i], rw_lnx_b[i])
        else:
            j = layer - N_A_LAYERS
            if j == 0:
                k_sh, v_sh = shared_kv(x, kv_g, kv_w, k_norm_g, pos)
            lam_init = 0.8 - 0.6 * math.exp(-0.3 * layer)
            h = rms_norm(x, g_mix[layer])
            x = x + diff_attention(h, k_sh, v_sh, pos, da_w_q[j], da_q_norm_g[j],
                                   da_lam_q1[j], da_lam_k1[j], da_lam_q2[j], da_lam_k2[j],
                                   da_subln_g[j], da_w_o[j], lam_init)
        h = rms_norm(x, g_ffn[layer])
        x = x + swiglu_ffn(h, ffn_w_gu[layer], ffn_w_down[layer])
    return x
```

```python
import contextlib
import numpy as np
import ml_dtypes
import concourse.bass as bass
import concourse.mybir as mybir
from concourse.bass_utils import run_bass_kernel_spmd

F32 = mybir.dt.float32
BF16 = mybir.dt.bfloat16
AF = mybir.ActivationFunctionType
ALU = mybir.AluOpType
AX = mybir.AxisListType

D = 1024
SEQ = 8192
BATCH = 4
DFF = 2816
C0 = float(np.exp(-0.5))


class Buf:
    __slots__ = ("name", "writers", "readers", "dsem", "accum")

    def __init__(self, name, dsem=None, accum=False):
        self.name = name
        self.writers = {}
        self.readers = {}
        self.dsem = dsem
        self.accum = accum


class DSem:
    def __init__(self, sem, key):
        self.sem = sem
        self.key = key
        self.count = 0


class Eng:
    def __init__(self, name, obj, sem):
        self.name = name
        self.obj = obj
        self.sem = sem
        self.count = 0
        self.waited = {}


class Tl:
    def __init__(self, t, b):
        self.t = t
        self.b = b


class P:
    def __init__(self, nc, stack):
        self.nc = nc
        self.stack = stack
        self.engs = {}
        for name, obj in (("pe", nc.tensor), ("act", nc.scalar), ("dve", nc.vector),
                          ("pool", nc.gpsimd), ("sp", nc.sync)):
            sem = stack.enter_context(nc.semaphore("s_" + name))
            self.engs[name] = Eng(name, obj, sem)
        self.n_dsem = 0
        self.n_ins = 0
        self.uid = 0
        self.dsems = []
        self.free_dsems = []
        self.root = stack

    def dsem(self, name):
        if self.free_dsems:
            d = self.free_dsems.pop()
        else:
            sem = self.root.enter_context(self.nc.semaphore("d_%d" % self.n_dsem))
            self.n_dsem += 1
            d = DSem(sem, "d%d" % self.n_dsem)
        self.dsems.append(d)
        return d

    def sb(self, name, shape, dtype, dma=False):
        self.uid += 1
        name = "%s_%d" % (name, self.uid)
        t = self.stack.enter_context(self.nc.sbuf_tensor(name, list(shape), dtype))
        return Tl(t, Buf(name, self.dsem(name) if dma else None))

    def ps(self, name, shape, dtype=F32):
        self.uid += 1
        name = "%s_%d" % (name, self.uid)
        t = self.stack.enter_context(self.nc.psum_tensor(name, list(shape), dtype))
        return Tl(t, Buf(name))

    def dram(self, name):
        return Buf(name, None, accum=True)

    def _wait_deps(self, eng, reads, writes):
        deps = {}

        def add(tok, war):
            if tok[2] == eng.name and (eng.name == "pe" or war):
                return
            key = tok[3]
            if key not in deps or deps[key][1] < tok[1]:
                deps[key] = tok

        for b in reads:
            for tok in b.writers.values():
                add(tok, False)
        for b in writes:
            for tok in b.writers.values():
                add(tok, False)
            for tok in b.readers.values():
                add(tok, True)
        for key, tok in deps.items():
            if eng.waited.get(key, 0) < tok[1]:
                eng.obj.wait_ge(tok[0], tok[1])
                eng.waited[key] = tok[1]

    def _update(self, tok, reads, writes):
        key = tok[3]
        for b in writes:
            if b.accum:
                b.writers[key] = tok
            else:
                b.writers = {key: tok}
                b.readers = {}
        for b in reads:
            b.readers[key] = tok

    def op(self, eng_name, fn, reads=(), writes=()):
        eng = self.engs[eng_name]
        self._wait_deps(eng, reads, writes)
        ins = fn(eng.obj)
        eng.count += 1
        ins.then_inc(eng.sem, 1)
        self._update((eng.sem, eng.count, eng.name, eng.name), reads, writes)
        self.n_ins += 1

    def dma(self, q_name, out, in_, reads=(), writes=(), dsem=None, **kw):
        eng = self.engs[q_name]
        if dsem is None:
            for b in list(writes) + list(reads):
                if b.dsem is not None:
                    dsem = b.dsem
                    break
        assert dsem is not None
        self._wait_deps(eng, reads, writes)
        ins = eng.obj.dma_start(out=out, in_=in_, **kw)
        dsem.count += 16
        ins.then_inc(dsem.sem, 16)
        self._update((dsem.sem, dsem.count, "dma", dsem.key), reads, writes)
        self.n_ins += 1

    def wait_all(self, eng_name, bufs):
        self._wait_deps(self.engs[eng_name], bufs, ())

    def barrier(self):
        for e in self.engs.values():
            for o in self.engs.values():
                if o is not e and o.count > e.waited.get(o.name, 0):
                    e.obj.wait_ge(o.sem, o.count)
                    e.waited[o.name] = o.count
            for d in self.dsems:
                if d.count > e.waited.get(d.key, 0):
                    e.obj.wait_ge(d.sem, d.count)
                    e.waited[d.key] = d.count

    @contextlib.contextmanager
    def scope(self):
        outer, outer_ds = self.stack, self.dsems
        with contextlib.ExitStack() as st:
            self.stack = st
            self.dsems = []
            try:
                yield
                self.barrier()
            finally:
                self.free_dsems.extend(self.dsems)
                self.stack, self.dsems = outer, outer_ds

    def mm(self, out, lhsT, rhs, start, stop, reads, writes):
        self.op("pe", lambda e: e.matmul(out, lhsT, rhs, start=start, stop=stop,
                                         skip_group_check=True), reads, writes)

    def tr(self, out, in_, ident, reads, writes):
        self.op("pe", lambda e: e.transpose(out, in_, ident), reads, writes)

    def act(self, out, in_, func, reads, writes, **kw):
        self.op("act", lambda e: e.activation(out, in_, func, **kw), reads, writes)

    def tt(self, eng, out, in0, in1, op, reads, writes):
        self.op(eng, lambda e: e.tensor_tensor(out, in0, in1, op), reads, writes)

    def ts(self, eng, out, in0, s1, s2, op0, op1, reads, writes):
        if s2 is None:
            self.op(eng, lambda e: e.tensor_scalar(out, in0, s1, None, op0), reads, writes)
        else:
            self.op(eng, lambda e: e.tensor_scalar(out, in0, s1, s2, op0, op1), reads, writes)

    def stt(self, eng, out, in0, scalar, in1, op0, op1, reads, writes):
        self.op("dve", lambda e: e.scalar_tensor_tensor(out, in0, scalar, in1, op0, op1), reads, writes)

    def cp(self, eng, out, in_, reads, writes):
        if eng == "act":
            self.op("act", lambda e: e.copy(out, in_), reads, writes)
        else:
            self.op(eng, lambda e: e.tensor_copy(out, in_), reads, writes)


def kmaj(W, kc):
    K, N = W.shape
    Wp = np.zeros((kc * 128, N), np.float32)
    Wp[:K] = W
    return np.ascontiguousarray(Wp.reshape(kc, 128, N).transpose(1, 0, 2).reshape(128, kc * N))


def pcols(vec, n):
    return np.ascontiguousarray(np.asarray(vec, np.float32).reshape(n, 128).T)


def consts_A():
    c = np.zeros((128, 900), np.float32)
    i = np.arange(128)
    blk = (i[:, None] // 64) == (i[None, :] // 64)
    c[:, 0:128] = np.eye(128)
    c[:, 128:256] = blk & (i[:, None] < i[None, :])
    c[:, 256:384] = blk & (i[:, None] > i[None, :])
    c[:, 384:512] = blk & (i[:, None] <= i[None, :])
    c[:, 512:640] = blk
    c[:, 640:896] = (np.arange(256) % 64 != 0)[None, :]
    c[:, 896] = 1.0
    return c


WA_COLS = 16640
O_WR, O_WK, O_WV, O_W1, O_A1, O_G1, O_W2, O_A2, O_G2A, O_G2B = (
    0, 4096, 8192, 12288, 12800, 13312, 14592, 15104, 15616, 16128)


def prep_A(inp, half):
    cs = slice(512 * half, 512 * half + 512)
    g = lambda k: np.asarray(inp[k][0], np.float32)
    wA = np.concatenate([
        kmaj(g("rw_w_r")[:, cs], 8), kmaj(g("rw_w_k")[:, cs], 8), kmaj(g("rw_w_v")[:, cs], 8),
        kmaj(g("rw_w1"), 8), kmaj(g("rw_a1"), 8), kmaj(g("rw_g1"), 8),
        kmaj(g("rw_w2")[:, cs], 1), kmaj(g("rw_a2")[:, cs], 1),
        kmaj(g("rw_g2")[0:128, cs], 1), kmaj(g("rw_g2")[128:160, cs], 1)], axis=1)
    assert wA.shape == (128, WA_COLS)
    pc = np.concatenate([
        pcols(inp["g_mix"][0], 8), pcols(g("rw_mu").reshape(-1), 48),
        pcols(g("rw_w0")[cs], 4), pcols(g("rw_a0")[cs], 4), pcols(g("rw_k_k")[cs], 4),
        pcols(g("rw_k_a")[cs], 4), pcols(g("rw_r_k").reshape(-1)[cs], 4)], axis=1)
    assert pc.shape == (128, 76)

    def stacked(vec):
        v = np.asarray(vec, np.float32)[cs].reshape(4, 2, 64)
        return np.concatenate([np.broadcast_to(v[None, :, 0, :], (64, 4, 64)),
                               np.broadcast_to(v[None, :, 1, :], (64, 4, 64))], 0).reshape(128, 256)

    lnx = np.concatenate([stacked(g("rw_lnx_w")), stacked(g("rw_lnx_b"))], axis=1)
    return {"wA": wA, "pcol": np.ascontiguousarray(pc), "cst": consts_A(),
            "lnx": np.ascontiguousarray(lnx)}


def emit_A(p, S, x_d, wA_d, pcol_d, cst_d, lnx_d, yg_d, yg_b, stop_after=None, dbg_d=None):
    nc = p.nc
    T = 256
    NT = S // T
    NQ = 4
    sb, ps = p.sb, p.ps

    cst = sb("cst", [128, 900], F32, dma=True)
    p.dma("sp", cst.t[:], cst_d, writes=[cst.b])
    pc = sb("pcol", [128, 80], F32, dma=True)
    p.dma("sp", pc.t[:, 0:76], pcol_d, writes=[pc.b])
    lnx = sb("lnx", [128, 512], F32, dma=True)
    p.dma("sp", lnx.t[:], lnx_d, writes=[lnx.b])
    ident = sb("ident", [128, 128], BF16)
    p.cp("dve", ident.t[:], cst.t[:, 0:128], [cst.b], [ident.b])
    bones = sb("bones", [128, 128], BF16)
    p.cp("dve", bones.t[:], cst.t[:, 512:640], [cst.b], [bones.b])
    onesc = sb("onesc", [128, 2], BF16)
    p.cp("dve", onesc.t[:, 0:1], cst.t[:, 896:897], [cst.b], [onesc.b])
    p.ts("dve", pc.t[:, 76:80], pc.t[:, 68:72], -1.0, 1.0, ALU.mult, ALU.add, [pc.b], [pc.b])
    maskSU = cst.t[:, 128:256].unsqueeze(1).to_broadcast([128, 4, 128])
    maskSL = cst.t[:, 256:384].unsqueeze(1).to_broadcast([128, 4, 128])
    maskIU = cst.t[:, 384:512].unsqueeze(1).to_broadcast([128, 4, 128])
    identB = cst.t[:, 0:128].unsqueeze(1).to_broadcast([128, 4, 128])
    rmask = cst.t[:, 640:896]

    W = sb("wA", [128, WA_COLS], BF16)
    stg = [sb("stg", [128, 2080], F32, dma=True) for _ in range(1)]
    for i in range(8):
        s_ = stg[0]
        p.dma("sp", s_.t[:], wA_d[:, i * 2080:(i + 1) * 2080], writes=[s_.b])
        p.cp("pool", W.t[:, i * 2080:(i + 1) * 2080], s_.t[:], [s_.b], [W.b])

    def wv(off, c, n, j0=0, jn=None):
        jn = n if jn is None else jn
        return W.t[:, off + c * n + j0: off + c * n + j0 + jn]

    xts = [sb("xt", [128, 1024], F32, dma=True) for _ in range(2)]
    junk = sb("junk", [128, 1024], BF16)
    ss = sb("ss", [128, 2], F32)
    rs = sb("rs", [128, 2], F32)
    hn = [sb("hn", [128, 1024], BF16) for _ in range(2)]
    hT = [sb("hT", [128, 8, T + 1], BF16) for _ in range(2)]
    hh = sb("hh", [128, 8, T], BF16)
    xm = [sb("xm", [128, 8, T], BF16) for _ in range(3)]
    th = sb("th", [128, T], BF16)
    al = sb("al", [128, T], BF16)
    sgA = sb("sgA", [128, T], BF16)
    sgB = sb("sgB", [128, T], BF16)
    for t_ in (th, al, sgB):
        p.op("pool", lambda e: e.memset(t_.t[:], 0.0), [], [t_.b])
    p.op("pool", lambda e: e.memset(hT[0].t[:, :, 0:1], 0.0), [], [hT[0].b])

    def pairtiles(nm, dt=F32):
        return [sb(nm, [128, T], dt) for _ in range(4)]

    rT = pairtiles("r")
    kf = pairtiles("kf")
    kk = pairtiles("kk")
    kk2 = pairtiles("kk2", BF16)
    aT = pairtiles("a")
    sg = pairtiles("sg")
    scr = {}
    for nm in ("csum", "cex", "d2", "einc", "eneg", "eexc", "ehat", "t1", "kmod", "bv", "rk", "rn", "kkn"):
        scr[nm] = [sb(nm, [128, T], F32) for _ in range(1)]

    blk = {}
    for nm in ("A", "R", "B", "K", "Bh", "Kh", "RK"):
        blk[nm] = sb("blk" + nm, [128, 4, NQ, 128], BF16)
        p.op("pool", lambda e: e.memset(blk[nm].t[:], 0.0), [], [blk[nm].b])
    WC = sb("WC", [128, 4, NQ], F32)
    BhT = sb("BhT", [128, NQ, 4, 128], BF16)
    KhT = sb("KhT", [128, NQ, 4, 128], BF16)
    Vst = sb("Vst", [128, NQ, 4, 64], BF16, dma=True)
    gst = sb("gst", [128, NQ, 4, 64], F32, dma=True)
    Vtok = sb("Vtok", [128, 512], BF16)
    gtok = sb("gtok", [128, 512], F32)
    Kakm = sb("Kakm", [128, NQ, 4, 128], BF16)
    Rbm = sb("Rbm", [128, NQ, 4, 128], BF16)
    Rkm = sb("Rkm", [128, NQ, 4, 128], BF16)
    TT = sb("TT", [128, NQ, 4, 128], BF16)
    Nw = [sb("Nw", [128, 4, 128], BF16) for _ in range(2)]
    Lw = [sb("Lw", [128, 4, 128], BF16) for _ in range(2)]
    Xw = [sb("Xw", [128, 4, 128], BF16) for _ in range(2)]
    M = sb("M", [128, 256], F32)
    Mbf = sb("Mbf", [128, 256], BF16)
    WM = sb("WM", [128, 256], F32)
    Mps = sb("Mps", [128, 256], F32)
    p.op("pool", lambda e: e.memset(M.t[:], 0.0), [], [M.b])
    p.op("pool", lambda e: e.memset(Mbf.t[:], 0.0), [], [Mbf.b])
    RHSsb = sb("RHSsb", [128, 256], BF16)
    Ust = sb("Ust", [128, 256], BF16)
    ysb = sb("ysb", [128, NQ, 256], F32)
    ysq = sb("ysq", [128, NQ, 256], F32, dma=True)
    st1 = sb("st1", [128, 16], F32)
    st2 = sb("st2", [128, 16], F32)
    st3 = sb("st3", [128, 16], F32)
    ssb = sb("ssb", [128, 16], F32)
    ygt = [sb("ygt", [128, NQ, 256], BF16, dma=True) for _ in range(2)]

    pj = [ps("pj", [128, 512]) for _ in range(3)]
    ptr = ps("ptr", [128, 1024], BF16)
    pg = [ps("pg", [128, 512]) for _ in range(2)]
    pq1 = ps("pq1", [128, 512])
    pq2 = ps("pq2", [128, 512])
    pjn = [0]
    pgn = [0]

    def nextpj():
        pjn[0] += 1
        return pj[pjn[0] % 3]

    def nextpg():
        pgn[0] += 1
        return pg[pgn[0] % 2]

    PC_G, PC_MU, PC_W0, PC_A0, PC_KK, PC_KA, PC_RK, PC_OMKA = 0, 8, 56, 60, 64, 68, 72, 76

    for it in range(NT):
        t0 = it * T
        cur, nxt = hT[it % 2], hT[(it + 1) % 2]
        for s in range(2):
            xt = xts[(2 * it + s) % 2]
            hn_ = hn[s]
            p.dma("sp", xt.t[:], x_d[t0 + s * 128: t0 + (s + 1) * 128, :], writes=[xt.b])
            p.act(junk.t[:], xt.t[:], AF.Square, [xt.b], [junk.b, ss.b], accum_out=ss.t[:, 0:1])
            p.ts("dve", rs.t[:, 0:1], ss.t[:, 0:1], 1.0 / D, 1e-6, ALU.mult, ALU.add, [ss.b], [rs.b])
            p.act(rs.t[:, 0:1], rs.t[:, 0:1], AF.Sqrt, [rs.b], [rs.b])
            p.op("dve", lambda e: e.reciprocal(rs.t[:, 0:1], rs.t[:, 0:1]), [rs.b], [rs.b])
            p.act(hn_.t[:], xt.t[:], AF.Copy, [xt.b, rs.b], [hn_.b], scale=rs.t[:, 0:1])
            for c in range(8):
                p.tr(ptr.t[:, c * 128:(c + 1) * 128], hn_.t[:, c * 128:(c + 1) * 128], ident.t[:],
                     [hn_.b, ident.b], [ptr.b])
            p.tt("dve", cur.t[:, :, 1 + s * 128: 1 + (s + 1) * 128],
                 ptr.t[:, :].rearrange("p (c t) -> p c t", c=8),
                 pc.t[:, PC_G:PC_G + 8].unsqueeze(2).to_broadcast([128, 8, 128]), ALU.mult,
                 [ptr.b, pc.b], [cur.b])
        p.cp("pool", nxt.t[:, :, 0:1], cur.t[:, :, T:T + 1], [cur.b], [nxt.b])
        if stop_after == 'A':
            continue
        p.tt("dve", hh.t[:], cur.t[:, :, 0:T], cur.t[:, :, 1:T + 1], ALU.subtract, [cur.b], [hh.b])

        def mix(m):
            x_ = xm[m % 3]
            for c in range(8):
                p.stt("dve" if c % 2 == 0 else "pool", x_.t[:, c, :], hh.t[:, c, :],
                      pc.t[:, PC_MU + m * 8 + c: PC_MU + m * 8 + c + 1], cur.t[:, c, 1:T + 1],
                      ALU.mult, ALU.add, [hh.b, cur.b, pc.b], [x_.b])
            return x_

        def proj(x_, off, n, j0, jn, out_ap, out_b):
            for c in range(8):
                p.mm(out_ap, wv(off, c, n, j0, jn), x_.t[:, c, :], c == 0, c == 7, [W.b, x_.b], [out_b])

        x_ = mix(0)
        for j in range(4):
            a_ = nextpj()
            proj(x_, O_WR, 512, j * 128, 128, a_.t[:, 0:T], a_.b)
            p.cp("act", rT[j].t[:], a_.t[:, 0:T], [a_.b], [rT[j].b])
        if stop_after == 'r':
            continue
        x_ = mix(1)
        a_ = nextpj()
        proj(x_, O_W1, 64, 0, 64, a_.t[0:64, 0:T], a_.b)
        p.act(th.t[0:64, :], a_.t[0:64, 0:T], AF.Tanh, [a_.b], [th.b])
        for j in range(4):
            a_ = nextpj()
            p.mm(a_.t[:, 0:T], W.t[:, O_W2 + j * 128: O_W2 + (j + 1) * 128], th.t[:], True, True,
                 [W.b, th.b], [a_.b])
            p.act(sg[j].t[:], a_.t[:, 0:T], AF.Sigmoid, [a_.b, pc.b], [sg[j].b],
                  bias=pc.t[:, PC_W0 + j: PC_W0 + j + 1])
        if stop_after == 'w':
            continue
        x_ = mix(2)
        for j in range(4):
            a_ = nextpj()
            proj(x_, O_WK, 512, j * 128, 128, a_.t[:, 0:T], a_.b)
            p.cp("act", kf[j].t[:], a_.t[:, 0:T], [a_.b], [kf[j].b])
            p.act(kk[j].t[:], a_.t[:, 0:T], AF.Copy, [a_.b, pc.b], [kk[j].b], scale=pc.t[:, PC_KK + j: PC_KK + j + 1])
            p.tt("dve", kk2[j].t[:], kk[j].t[:], kk[j].t[:], ALU.mult, [kk[j].b], [kk2[j].b])
        if stop_after == 'k':
            continue
        x_ = mix(3)
        for s2 in range(2):
            a_ = nextpj()
            for c in range(8):
                p.mm(a_.t[:, :], x_.t[:, c, s2 * 128:(s2 + 1) * 128], wv(O_WV, c, 512), c == 0, c == 7,
                     [W.b, x_.b], [a_.b])
            p.cp("act", Vtok.t[:], a_.t[:, :], [a_.b], [Vtok.b])
            src = Vtok.t[:, :].rearrange("p (j h v) -> p j h v", j=4, h=2)
            for qq in range(2):
                for hf in range(2):
                    p.dma("sp", Vst.t[hf * 64:(hf + 1) * 64, s2 * 2 + qq, :, :],
                          src[qq * 64:(qq + 1) * 64, :, hf, :], reads=[Vtok.b], writes=[Vst.b], dsem=Vst.b.dsem)
        if stop_after == 'v':
            continue
        x_ = mix(4)
        a_ = nextpj()
        proj(x_, O_A1, 64, 0, 64, a_.t[0:64, 0:T], a_.b)
        p.cp("act", al.t[0:64, :], a_.t[0:64, 0:T], [a_.b], [al.b])
        for j in range(4):
            a_ = nextpj()
            p.mm(a_.t[:, 0:T], W.t[:, O_A2 + j * 128: O_A2 + (j + 1) * 128], al.t[:], True, True,
                 [W.b, al.b], [a_.b])
            p.act(aT[j].t[:], a_.t[:, 0:T], AF.Sigmoid, [a_.b, pc.b], [aT[j].b],
                  bias=pc.t[:, PC_A0 + j: PC_A0 + j + 1])
        if stop_after == 'a':
            continue
        x_ = mix(5)
        a_ = nextpj()
        proj(x_, O_G1, 160, 0, 128, a_.t[:, 0:T], a_.b)
        p.act(sgA.t[:], a_.t[:, 0:T], AF.Sigmoid, [a_.b], [sgA.b])
        a_ = nextpj()
        proj(x_, O_G1, 160, 128, 32, a_.t[0:32, 0:T], a_.b)
        p.act(sgB.t[0:32, :], a_.t[0:32, 0:T], AF.Sigmoid, [a_.b], [sgB.b])
        for s2 in range(2):
            a_ = nextpj()
            p.mm(a_.t[:, :], sgA.t[:, s2 * 128:(s2 + 1) * 128], W.t[:, O_G2A:O_G2A + 512], True, False,
                 [W.b, sgA.b], [a_.b])
            p.mm(a_.t[:, :], sgB.t[:, s2 * 128:(s2 + 1) * 128], W.t[:, O_G2B:O_G2B + 512], False, True,
                 [W.b, sgB.b], [a_.b])
            p.cp("act", gtok.t[:], a_.t[:, :], [a_.b], [gtok.b])
            src = gtok.t[:, :].rearrange("p (j h v) -> p j h v", j=4, h=2)
            for qq in range(2):
                for hf in range(2):
                    p.dma("sp", gst.t[hf * 64:(hf + 1) * 64, s2 * 2 + qq, :, :],
                          src[qq * 64:(qq + 1) * 64, :, hf, :], reads=[gtok.b], writes=[gst.b], dsem=gst.b.dsem)

        if stop_after == 'proj':
            continue
        for j in range(4):
            S_ = {k_: v_[0] for k_, v_ in scr.items()}
            a_ = nextpj()
            p.mm(a_.t[:, 0:T], bones.t[:], kk2[j].t[:], True, True, [bones.b, kk2[j].b], [a_.b])
            p.act(S_["rn"].t[:], a_.t[:, 0:T], AF.Sqrt, [a_.b], [S_["rn"].b])
            p.ts("dve", S_["rn"].t[:], S_["rn"].t[:], 1e-12, None, ALU.max, None, [S_["rn"].b], [S_["rn"].b])
            p.op("dve", lambda e: e.reciprocal(S_["rn"].t[:], S_["rn"].t[:]), [S_["rn"].b], [S_["rn"].b])
            p.tt("dve", S_["kkn"].t[:], kk[j].t[:], S_["rn"].t[:], ALU.mult, [kk[j].b, S_["rn"].b], [S_["kkn"].b])
            p.op("dve", lambda e: e.tensor_tensor_scan(S_["csum"].t[:], rmask, sg[j].t[:], 0.0, ALU.mult, ALU.add),
                 [cst.b, sg[j].b], [S_["csum"].b])
            p.tt("pool", S_["cex"].t[:], S_["csum"].t[:], sg[j].t[:], ALU.subtract,
                 [S_["csum"].b, sg[j].b], [S_["cex"].b])
            csC = S_["csum"].t[:, 63::64]
            p.tt("dve", S_["d2"].t[:].rearrange("p (q t) -> p q t", q=NQ),
                 S_["csum"].t[:].rearrange("p (q t) -> p q t", q=NQ),
                 csC.unsqueeze(2).to_broadcast([128, NQ, 64]), ALU.subtract, [S_["csum"].b], [S_["d2"].b])
            p.act(S_["einc"].t[:], S_["csum"].t[:], AF.Exp, [S_["csum"].b], [S_["einc"].b], scale=-C0)
            p.act(S_["eneg"].t[:], S_["csum"].t[:], AF.Exp, [S_["csum"].b], [S_["eneg"].b], scale=C0)
            p.act(S_["eexc"].t[:], S_["cex"].t[:], AF.Exp, [S_["cex"].b], [S_["eexc"].b], scale=-C0)
            p.act(S_["ehat"].t[:], S_["d2"].t[:], AF.Exp, [S_["d2"].b], [S_["ehat"].b], scale=C0)
            p.act(WC.t[:, j, :], csC, AF.Exp, [S_["csum"].b], [WC.b], scale=-C0)
            p.act(S_["t1"].t[:], aT[j].t[:], AF.Identity, [aT[j].b, pc.b], [S_["t1"].b],
                  scale=pc.t[:, PC_KA + j: PC_KA + j + 1], bias=pc.t[:, PC_OMKA + j: PC_OMKA + j + 1])
            p.tt("dve", S_["kmod"].t[:], S_["t1"].t[:], kf[j].t[:], ALU.mult, [S_["t1"].b, kf[j].b], [S_["kmod"].b])
            p.tt("pool", S_["bv"].t[:], S_["kkn"].t[:], aT[j].t[:], ALU.mult, [S_["kkn"].b, aT[j].b], [S_["bv"].b])
            p.tt("pool", S_["rk"].t[:], rT[j].t[:], S_["kmod"].t[:], ALU.mult, [rT[j].b, S_["kmod"].b], [S_["rk"].b])

            def halves(name, fn):
                for hf in range(2):
                    psl = slice(hf * 64, hf * 64 + 64)
                    out = blk[name].t[psl, j, :, hf * 64: hf * 64 + 64]
                    fn("pool" if hf == 0 else "dve", out, psl)

            v3 = lambda tl, psl: tl.t[psl, :].rearrange("p (q t) -> p q t", q=NQ)
            halves("RK", lambda eng, out, psl: p.tt(
                eng, out, v3(S_["rk"], psl), pc.t[psl, PC_RK + j: PC_RK + j + 1].unsqueeze(2).to_broadcast([64, NQ, 64]),
                ALU.mult, [S_["rk"].b, pc.b], [blk["RK"].b]))
            halves("A", lambda eng, out, psl: p.stt(
                eng, out, v3(S_["kkn"], psl), -1.0, v3(S_["eexc"], psl), ALU.mult, ALU.mult,
                [S_["kkn"].b, S_["eexc"].b], [blk["A"].b]))
            halves("R", lambda eng, out, psl: p.tt(
                eng, out, v3(rT[j], psl), v3(S_["einc"], psl), ALU.mult, [rT[j].b, S_["einc"].b], [blk["R"].b]))
            halves("B", lambda eng, out, psl: p.tt(
                eng, out, v3(S_["bv"], psl), v3(S_["eneg"], psl), ALU.mult, [S_["bv"].b, S_["eneg"].b], [blk["B"].b]))
            halves("K", lambda eng, out, psl: p.tt(
                eng, out, v3(S_["kmod"], psl), v3(S_["eneg"], psl), ALU.mult, [S_["kmod"].b, S_["eneg"].b], [blk["K"].b]))
            halves("Bh", lambda eng, out, psl: p.tt(
                eng, out, v3(S_["bv"], psl), v3(S_["ehat"], psl), ALU.mult, [S_["bv"].b, S_["ehat"].b], [blk["Bh"].b]))
            halves("Kh", lambda eng, out, psl: p.tt(
                eng, out, v3(S_["kmod"], psl), v3(S_["ehat"], psl), ALU.mult, [S_["kmod"].b, S_["ehat"].b], [blk["Kh"].b]))

        if stop_after == 'D':
            continue
        for q in range(NQ):
            for nm, dst in (("Bh", BhT), ("Kh", KhT)):
                half = 0 if nm == "Bh" else 512
                for j in range(4):
                    p.tr(ptr.t[:, half + j * 128: half + (j + 1) * 128], blk[nm].t[:, j, q, :], ident.t[:],
                         [blk[nm].b, ident.b], [ptr.b])
            p.cp("act", BhT.t[:, q, :, :], ptr.t[:, 0:512].rearrange("p (j t) -> p j t", j=4), [ptr.b], [BhT.b])
            p.cp("act", KhT.t[:, q, :, :], ptr.t[:, 512:1024].rearrange("p (j t) -> p j t", j=4), [ptr.b], [KhT.b])

            def gprod(l, r, mask, dst_ap, dst_b, eng="dve"):
                g_ = nextpg()
                for j in range(4):
                    p.mm(g_.t[:, j * 128:(j + 1) * 128], blk[l].t[:, j, q, :], blk[r].t[:, j, q, :], True, True,
                         [blk[l].b, blk[r].b], [g_.b])
                p.tt(eng, dst_ap, g_.t[:, :].rearrange("p (j t) -> p j t", j=4), mask, ALU.mult,
                     [g_.b, cst.b], [dst_b])

            gprod("K", "A", maskSU, Kakm.t[:, q, :, :], Kakm.b)
            gprod("B", "R", maskIU, Rbm.t[:, q, :, :], Rbm.b)
            gprod("K", "R", maskIU, Rkm.t[:, q, :, :], Rkm.b)
            gprod("B", "A", maskSU, Nw[0].t[:], Nw[0].b)
            gprod("A", "B", maskSL, Lw[0].t[:], Lw[0].b)
            p.tt("dve", Xw[0].t[:], Nw[0].t[:], identB, ALU.add, [Nw[0].b, cst.b], [Xw[0].b])
            for i in range(1, 6):
                po, pn = (i - 1) % 2, i % 2
                if i < 5:
                    g_ = nextpg()
                    for j in range(4):
                        p.mm(g_.t[:, j * 128:(j + 1) * 128], Lw[po].t[:, j, :], Nw[po].t[:, j, :], True, True,
                             [Lw[po].b, Nw[po].b], [g_.b])
                    p.cp("act", Nw[pn].t[:], g_.t[:, :].rearrange("p (j t) -> p j t", j=4), [g_.b], [Nw[pn].b])
                g_ = nextpg()
                for j in range(4):
                    p.mm(g_.t[:, j * 128:(j + 1) * 128], Nw[po].t[:, j, :], Lw[po].t[:, j, :], True, True,
                         [Lw[po].b, Nw[po].b], [g_.b])
                p.cp("act", Lw[pn].t[:], g_.t[:, :].rearrange("p (j t) -> p j t", j=4), [g_.b], [Lw[pn].b])
                g_ = nextpg()
                for j in range(4):
                    p.mm(g_.t[:, j * 128:(j + 1) * 128], Lw[pn].t[:, j, :], Xw[po].t[:, j, :], True, True,
                         [Lw[pn].b, Xw[po].b], [g_.b])
                dst_ap, dst_b = (TT.t[:, q, :, :], TT.b) if i == 5 else (Xw[pn].t[:], Xw[pn].b)
                p.tt("dve", dst_ap, g_.t[:, :].rearrange("p (j t) -> p j t", j=4), Xw[po].t[:], ALU.add,
                     [g_.b, Xw[po].b], [dst_b])

        if stop_after == 'G':
            continue
        for q in range(NQ):
            jc = lambda j: slice(j * 64, j * 64 + 64)
            p.tt("dve", WM.t[:].rearrange("p (j v) -> p j v", j=4), M.t[:].rearrange("p (j v) -> p j v", j=4),
                 WC.t[:, :, q:q + 1].to_broadcast([128, 4, 64]), ALU.mult, [M.b, WC.b], [WM.b])
            for j in range(4):
                p.mm(pq1.t[:, jc(j)], blk["A"].t[:, j, q, :], Mbf.t[:, jc(j)], True, False,
                     [blk["A"].b, Mbf.b], [pq1.b])
                p.mm(pq1.t[:, jc(j)], Kakm.t[:, q, j, :], Vst.t[:, q, j, :], False, True,
                     [Kakm.b, Vst.b], [pq1.b])
            p.cp("act", RHSsb.t[:], pq1.t[:, 0:256], [pq1.b], [RHSsb.b])
            import os
            hd = int(os.environ.get("HDBG", "9"))
            if hd <= 1:
                continue
            for j in range(4):
                p.mm(pq1.t[:, 256 + j * 64: 256 + (j + 1) * 64], TT.t[:, q, j, :], RHSsb.t[:, jc(j)], True, True,
                     [TT.b, RHSsb.b], [pq1.b])
            p.cp("dve", Ust.t[:], pq1.t[:, 256:512], [pq1.b], [Ust.b])
            if hd <= 2:
                continue
            for j in range(4):
                p.mm(pq2.t[:, jc(j)], blk["R"].t[:, j, q, :], Mbf.t[:, jc(j)], True, False,
                     [blk["R"].b, Mbf.b], [pq2.b])
                p.mm(pq2.t[:, jc(j)], Rkm.t[:, q, j, :], Vst.t[:, q, j, :], False, False, [Rkm.b, Vst.b], [pq2.b])
                p.mm(pq2.t[:, jc(j)], Rbm.t[:, q, j, :], Ust.t[:, jc(j)], False, True, [Rbm.b, Ust.b], [pq2.b])
            if hd <= 3:
                p.cp("act", ysb.t[:, q, :], pq2.t[:, 0:256], [pq2.b], [ysb.b])
                continue
            for j in range(4):
                o_ = pq2.t[:, 256 + j * 64: 256 + (j + 1) * 64]
                p.mm(o_, KhT.t[:, q, j, :], Vst.t[:, q, j, :], True, False, [KhT.b, Vst.b], [pq2.b])
                p.mm(o_, BhT.t[:, q, j, :], Ust.t[:, jc(j)], False, True, [BhT.b, Ust.b], [pq2.b])
            if hd <= 4:
                p.cp("act", ysb.t[:, q, :], pq2.t[:, 0:256], [pq2.b], [ysb.b])
                if q == 0 and it == 0 and dbg_d is not None:
                    dbgt = Tl(ysq.t[:].rearrange("p q v -> p (q v)"), ysq.b)
                    p.cp("act", dbgt.t[:, 0:256], pq2.t[:, 256:512], [pq2.b], [dbgt.b])
                    p.cp("act", dbgt.t[:, 256:512], Ust.t[:], [Ust.b], [dbgt.b])
                    p.cp("act", dbgt.t[:, 512:768], RHSsb.t[:], [RHSsb.b], [dbgt.b])
                    p.cp("act", dbgt.t[:, 768:896], TT.t[:, 0, 0, :], [TT.b], [dbgt.b])
                    p.cp("act", dbgt.t[:, 896:1024], BhT.t[:, 0, 0, :], [BhT.b], [dbgt.b])
                    p.dma("pool", dbg_d, dbgt.t, reads=[dbgt.b], writes=[yg_b])
                continue
            if hd == 5:
                p.cp("dve", M.t[:], WM.t[:], [pq2.b, WM.b], [M.b])
            else:
                p.cp("act", Mps.t[:], pq2.t[:, 256:512], [pq2.b], [Mps.b])
                p.tt("dve", M.t[:], Mps.t[:], WM.t[:], ALU.add, [Mps.b, WM.b], [M.b])
            if hd != 6:
                p.cp("dve", Mbf.t[:], M.t[:], [M.b], [Mbf.b])
            p.cp("act", ysb.t[:, q, :], pq2.t[:, 0:256], [pq2.b], [ysb.b])

        if stop_after == 'H':
            continue
        a_ = nextpj()
        for q in range(NQ):
            for j in range(4):
                p.mm(a_.t[:, q * 4 + j: q * 4 + j + 1], blk["RK"].t[:, j, q, :], onesc.t[:, 0:1], True, True,
                     [blk["RK"].b, onesc.b], [a_.b])
        p.cp("act", ssb.t[:], a_.t[:, 0:16], [a_.b], [ssb.b])
        y3 = ysb.t[:].rearrange("p q (j v) -> p (q j) v", j=4)
        q3 = ysq.t[:].rearrange("p q (j v) -> p (q j) v", j=4)
        p.op("dve", lambda e: e.tensor_reduce(st1.t[:], y3, AX.X, ALU.add), [ysb.b], [st1.b])
        p.act(ysq.t[:], ysb.t[:], AF.Square, [ysb.b], [ysq.b])
        p.op("dve", lambda e: e.tensor_reduce(st2.t[:], q3, AX.X, ALU.add), [ysq.b], [st2.b])
        p.ts("dve", st1.t[:], st1.t[:], 1.0 / 64, None, ALU.mult, None, [st1.b], [st1.b])
        p.tt("dve", st3.t[:], st1.t[:], st1.t[:], ALU.mult, [st1.b], [st3.b])
        p.stt("dve", st2.t[:], st2.t[:], 1.0 / 64, st3.t[:], ALU.mult, ALU.subtract, [st2.b, st3.b], [st2.b])
        p.ts("dve", st2.t[:], st2.t[:], 64e-5, None, ALU.add, None, [st2.b], [st2.b])
        p.act(st2.t[:], st2.t[:], AF.Sqrt, [st2.b], [st2.b])
        p.op("dve", lambda e: e.reciprocal(st2.t[:], st2.t[:]), [st2.b], [st2.b])
        bc = lambda tl: tl.t[:, :].unsqueeze(2).to_broadcast([128, 16, 64])
        p.tt("dve", y3, y3, bc(st1), ALU.subtract, [ysb.b, st1.b], [ysb.b])
        p.tt("dve", y3, y3, bc(st2), ALU.mult, [ysb.b, st2.b], [ysb.b])
        lw = lnx.t[:, 0:256].unsqueeze(1).to_broadcast([128, NQ, 256])
        lb = lnx.t[:, 256:512].unsqueeze(1).to_broadcast([128, NQ, 256])
        p.tt("dve", ysb.t[:], ysb.t[:], lw, ALU.mult, [ysb.b, lnx.b], [ysb.b])
        p.tt("dve", ysb.t[:], ysb.t[:], lb, ALU.add, [ysb.b, lnx.b], [ysb.b])
        p.tt("dve", q3, Vst.t[:].rearrange("p q j v -> p (q j) v"), bc(ssb), ALU.mult, [Vst.b, ssb.b], [ysq.b])
        p.tt("dve", ysb.t[:], ysb.t[:], ysq.t[:], ALU.add, [ysb.b, ysq.b], [ysb.b])
        yo = ygt[it % 2]
        p.tt("dve", yo.t[:], ysb.t[:], gst.t[:].rearrange("p q j v -> p q (j v)"), ALU.mult,
             [ysb.b, gst.b], [yo.b])
        dv = yg_d[t0:t0 + T, :].rearrange("(q p) (j h v) -> p q j h v", p=64, j=4, h=2)
        for hf in range(2):
            for q in range(NQ):
                p.dma("pool", dv[:, q, :, hf, :],
                      yo.t[hf * 64:(hf + 1) * 64, q, :].rearrange("p (j v) -> p j v", j=4),
                      reads=[yo.b], writes=[yg_b])


def build_A(S, stop_after=None):
    nc = bass.Bass("TRN2", target_bir_lowering=False)
    x_d = nc.dram_tensor("x", [S, D], F32, kind="ExternalInput").ap()
    wA_d = nc.dram_tensor("wA", [128, WA_COLS], F32, kind="ExternalInput").ap()
    pcol_d = nc.dram_tensor("pcol", [128, 76], F32, kind="ExternalInput").ap()
    cst_d = nc.dram_tensor("cst", [128, 900], F32, kind="ExternalInput").ap()
    lnx_d = nc.dram_tensor("lnx", [128, 512], F32, kind="ExternalInput").ap()
    yg_d = nc.dram_tensor("yg", [S, 512], BF16, kind="ExternalOutput").ap()
    dbg_d = None
    with contextlib.ExitStack() as st:
        p = P(nc, st)
        yg_b = p.dram("yg")
        emit_A(p, S, x_d, wA_d, pcol_d, cst_d, lnx_d, yg_d, yg_b, stop_after, dbg_d)
        p.wait_all("pool", [yg_b])
        print("phase A instructions:", p.n_ins)
    return nc


def load_weights_bf16(p, W, w_d, ncols, piece, stg):
    n = ncols // piece
    assert n * piece == ncols
    for i in range(n):
        p.dma("sp", stg.t[:, 0:piece], w_d[:, i * piece:(i + 1) * piece], writes=[stg.b])
        p.cp("pool", W.t[:, i * piece:(i + 1) * piece], stg.t[:, 0:piece], [stg.b], [W.b])


def emit_rms_T(p, xt, hn, ss, rs, junk, ptr, ident, eps):
    p.act(junk.t[:], xt.t[:], AF.Square, [xt.b], [junk.b, ss.b], accum_out=ss.t[:, 0:1])
    p.ts("dve", rs.t[:, 0:1], ss.t[:, 0:1], 1.0 / D, eps, ALU.mult, ALU.add, [ss.b], [rs.b])
    p.act(rs.t[:, 0:1], rs.t[:, 0:1], AF.Sqrt, [rs.b], [rs.b])
    p.op("dve", lambda e: e.reciprocal(rs.t[:, 0:1], rs.t[:, 0:1]), [rs.b], [rs.b])
    p.act(hn.t[:], xt.t[:], AF.Copy, [xt.b, rs.b], [hn.b], scale=rs.t[:, 0:1])
    for c in range(8):
        p.tr(ptr.t[:, c * 128:(c + 1) * 128], hn.t[:, c * 128:(c + 1) * 128], ident.t[:],
             [hn.b, ident.b], [ptr.b])


def emit_projres(p, NTOK, a_d, w_d, id_d, xres_d, xout_d, xout_b, in_bufs=()):
    with p.scope():
        sb, ps = p.sb, p.ps
        idf = sb("idf", [128, 128], F32, dma=True)
        p.dma("sp", idf.t[:], id_d, writes=[idf.b])
        ident = sb("ident", [128, 128], BF16)
        p.cp("dve", ident.t[:], idf.t[:], [idf.b], [ident.b])
        W = sb("Wo", [128, 8192], BF16)
        stg = sb("stg", [128, 2048], F32, dma=True)
        load_weights_bf16(p, W, w_d, 8192, 2048, stg)
        ats = [sb("at", [128, 1024], BF16, dma=True) for _ in range(2)]
        xrs = [sb("xr", [128, 1024], F32, dma=True) for _ in range(2)]
        aT = [sb("aT", [128, 8, 128], BF16) for _ in range(2)]
        osb = [sb("osb", [128, 512], F32) for _ in range(2)]
        xo = [sb("xo", [128, 1024], F32, dma=True) for _ in range(2)]
        ptr = [ps("ptr", [128, 1024], BF16) for _ in range(2)]
        pj = [ps("pj", [128, 512]) for _ in range(4)]
        n = 0
        for i in range(NTOK // 128):
            at, xr, aT_, xo_, ptr_ = ats[i % 2], xrs[i % 2], aT[i % 2], xo[i % 2], ptr[i % 2]
            rows = slice(i * 128, (i + 1) * 128)
            p.dma("sp", at.t[:], a_d[rows, :], reads=list(in_bufs), writes=[at.b])
            p.dma("sp", xr.t[:], xres_d[rows, :], reads=list(in_bufs), writes=[xr.b])
            for c in range(8):
                p.tr(ptr_.t[:, c * 128:(c + 1) * 128], at.t[:, c * 128:(c + 1) * 128], ident.t[:],
                     [at.b, ident.b], [ptr_.b])
            p.cp("dve", aT_.t[:], ptr_.t[:, :].rearrange("p (c t) -> p c t", c=8), [ptr_.b], [aT_.b])
            for hc in range(2):
                a_ = pj[n % 4]
                o_ = osb[n % 2]
                n += 1
                for c in range(8):
                    p.mm(a_.t[:, :], aT_.t[:, c, :], W.t[:, c * 1024 + hc * 512: c * 1024 + (hc + 1) * 512],
                         c == 0, c == 7, [aT_.b, W.b], [a_.b])
                p.cp("act", o_.t[:], a_.t[:, :], [a_.b], [o_.b])
                p.tt("pool", xo_.t[:, hc * 512:(hc + 1) * 512], o_.t[:], xr.t[:, hc * 512:(hc + 1) * 512], ALU.add,
                     [o_.b, xr.b], [xo_.b])
            p.dma("pool", xout_d[rows, :], xo_.t[:], reads=[xo_.b], writes=[xout_b])


def emit_ffn(p, NTOK, x_d, gcol_d, id_d, wgu_d, wd_d, xout_d, xout_b, in_bufs=()):
    NF = DFF // 128
    with p.scope():
        sb, ps = p.sb, p.ps
        idf = sb("idf", [128, 136], F32, dma=True)
        p.dma("sp", idf.t[:, 0:128], id_d, writes=[idf.b])
        p.dma("sp", idf.t[:, 128:136], gcol_d, writes=[idf.b])
        ident = sb("ident", [128, 128], BF16)
        p.cp("dve", ident.t[:], idf.t[:, 0:128], [idf.b], [ident.b])
        Wgu = sb("Wgu", [128, 8 * 2 * DFF], BF16)
        Wd = sb("Wd", [128, NF * 1024], BF16)
        stg = sb("stg", [128, 2816], F32, dma=True)
        load_weights_bf16(p, Wgu, wgu_d, 8 * 2 * DFF, 2816, stg)
        load_weights_bf16(p, Wd, wd_d, NF * 1024, 2816, stg)
        xg = sb("xg", [128, 4, 1024], F32, dma=True)
        junk = sb("junk", [128, 1024], BF16)
        ss = sb("ss", [128, 2], F32)
        rs = sb("rs", [128, 2], F32)
        hn = sb("hn", [128, 1024], BF16)
        hT = sb("hT", [128, 8, 512], BF16)
        act = sb("act", [128, NF, 512], BF16)
        sg = [sb("sg", [128, 512], BF16) for _ in range(2)]
        osb = [sb("osb", [128, 512], F32) for _ in range(2)]
        xo = [sb("xo", [128, 1024], F32, dma=True) for _ in range(2)]
        ptr = ps("ptr", [128, 1024], BF16)
        pg = [ps("pg", [128, 512]) for _ in range(2)]
        pu = [ps("pu", [128, 512]) for _ in range(2)]
        pd = [ps("pd", [128, 512]) for _ in range(2)]
        gB = idf.t[:, 128:136].unsqueeze(2).to_broadcast([128, 8, 128])
        nd = 0
        for gi in range(NTOK // 512):
            for s in range(4):
                rows = slice(gi * 512 + s * 128, gi * 512 + (s + 1) * 128)
                xt = Tl(xg.t[:, s, :], xg.b)
                p.dma("sp", xg.t[:, s, :], x_d[rows, :], reads=list(in_bufs), writes=[xg.b])
                emit_rms_T(p, xt, hn, ss, rs, junk, ptr, ident, 1e-6)
                p.tt("dve", hT.t[:, :, s * 128:(s + 1) * 128], ptr.t[:, :].rearrange("p (c t) -> p c t", c=8),
                     gB, ALU.mult, [ptr.b, idf.b], [hT.b])
            for fc in range(NF):
                g_, u_, s_ = pg[fc % 2], pu[fc % 2], sg[fc % 2]
                for c in range(8):
                    p.mm(g_.t[:, :], Wgu.t[:, c * 5632 + fc * 128: c * 5632 + (fc + 1) * 128], hT.t[:, c, :],
                         c == 0, c == 7, [Wgu.b, hT.b], [g_.b])
                for c in range(8):
                    p.mm(u_.t[:, :], Wgu.t[:, c * 5632 + DFF + fc * 128: c * 5632 + DFF + (fc + 1) * 128],
                         hT.t[:, c, :], c == 0, c == 7, [Wgu.b, hT.b], [u_.b])
                p.act(s_.t[:], g_.t[:, :], AF.Silu, [g_.b], [s_.b])
                p.tt("dve", act.t[:, fc, :], u_.t[:, :], s_.t[:], ALU.mult, [u_.b, s_.b], [act.b])
            for s in range(4):
                rows = slice(gi * 512 + s * 128, gi * 512 + (s + 1) * 128)
                xo_ = xo[s % 2]
                for hc in range(2):
                    d_, o_ = pd[nd % 2], osb[nd % 2]
                    nd += 1
                    for fc in range(NF):
                        p.mm(d_.t[:, :], act.t[:, fc, s * 128:(s + 1) * 128],
                             Wd.t[:, fc * 1024 + hc * 512: fc * 1024 + (hc + 1) * 512],
                             fc == 0, fc == NF - 1, [act.b, Wd.b], [d_.b])
                    p.cp("act", o_.t[:], d_.t[:, :], [d_.b], [o_.b])
                    p.tt("pool", xo_.t[:, hc * 512:(hc + 1) * 512], o_.t[:], xg.t[:, s, hc * 512:(hc + 1) * 512],
                         ALU.add, [o_.b, xg.b], [xo_.b])
                p.dma("pool", xout_d[rows, :], xo_.t[:], reads=[xo_.b], writes=[xout_b])


def build_B(NTOK, final=False):
    nc = bass.Bass("TRN2", target_bir_lowering=False)
    a_d = nc.dram_tensor("a", [NTOK, D], BF16, kind="ExternalInput").ap()
    xres_d = nc.dram_tensor("xres", [NTOK, D], F32, kind="ExternalInput").ap()
    wo_d = nc.dram_tensor("wo", [128, 8192], F32, kind="ExternalInput").ap()
    id_d = nc.dram_tensor("ident", [128, 128], F32, kind="ExternalInput").ap()
    gcol_d = nc.dram_tensor("gcol", [128, 8], F32, kind="ExternalInput").ap()
    wgu_d = nc.dram_tensor("wgu", [128, 8 * 2 * DFF], F32, kind="ExternalInput").ap()
    wd_d = nc.dram_tensor("wd", [128, 22 * 1024], F32, kind="ExternalInput").ap()
    x1_d = nc.dram_tensor("x1", [NTOK, D], F32, kind="Internal").ap()
    x2_d = nc.dram_tensor("x2", [NTOK, D], F32, kind="ExternalOutput").ap()
    with contextlib.ExitStack() as st:
        p = P(nc, st)
        x1_b, x2_b = p.dram("x1"), p.dram("x2")
        emit_projres(p, NTOK, a_d, wo_d, id_d, xres_d, x1_d, x1_b)
        emit_ffn(p, NTOK, x1_d, gcol_d, id_d, wgu_d, wd_d, x2_d, x2_b, in_bufs=[x1_b])
        p.wait_all("pool", [x2_b])
        print("phase B instructions:", p.n_ins)
    return nc


def prep_B(w_o, g_ffn, w_gu, w_down):
    return {"wo": kmaj(np.asarray(w_o, np.float32), 8), "ident": np.eye(128, dtype=np.float32),
            "gcol": pcols(g_ffn, 8), "wgu": kmaj(np.asarray(w_gu, np.float32), 8),
            "wd": kmaj(np.asarray(w_down, np.float32), 22)}


LAM_INIT = 0.8 - 0.6 * float(np.exp(-0.3 * 1))
T_QG, T_KG, T_SG, T_LAM, T_COLS = 0, 512, 1024, 1152, 1408


def prep_C(inp, half, S):
    cs = slice(512 * half, 512 * half + 512)
    f = lambda a: np.asarray(a, np.float32)
    rep = lambda v, n: np.broadcast_to(np.tile(f(v), n)[None, :], (128, len(v) * n))
    tabs = np.concatenate([rep(inp["da_q_norm_g"][0], 8), rep(inp["k_norm_g"], 8), rep(inp["da_subln_g"][0], 1),
                           rep(inp["da_lam_q1"][0], 1), rep(inp["da_lam_k1"][0], 1),
                           rep(inp["da_lam_q2"][0], 1), rep(inp["da_lam_k2"][0], 1)], axis=1)
    assert tabs.shape == (128, T_COLS)
    inv = np.power(np.float32(500000.0), -np.arange(0, 16, 2, dtype=np.float32) / np.float32(16))
    ang = np.arange(S, dtype=np.float32)[:, None] * inv[None, :].astype(np.float32)
    lay = lambda t: np.ascontiguousarray(t.reshape(S // 128, 128, 8).transpose(1, 0, 2).reshape(128, S // 16))
    kvw = f(inp["kv_w"])
    return {"wq": kmaj(f(inp["da_w_q"][0])[:, cs], 8), "wk": kmaj(kvw[:, cs], 8),
            "wv": kmaj(kvw[:, 1024 + 512 * half: 1024 + 512 * half + 512], 8),
            "gcols": np.ascontiguousarray(np.concatenate([pcols(inp["g_mix"][1], 8), pcols(inp["kv_g"], 8)], 1)),
            "tabs": np.ascontiguousarray(tabs), "ident": np.eye(128, dtype=np.float32),
            "cos": lay(np.cos(ang).astype(np.float32)), "sin": lay(np.sin(ang).astype(np.float32))}


def emit_C(p, S, x1_d, wq_d, wk_d, wv_d, gcols_d, tabs_d, id_d, cos_d, sin_d, QT_d, KT_d, V_d, att_d, att_b,
           in_bufs=()):
    NTL = S // 128
    scr_b = p.dram("qkv_scratch")
    with p.scope():
        sb, ps = p.sb, p.ps
        idf = sb("idf", [128, 144], F32, dma=True)
        p.dma("sp", idf.t[:, 0:128], id_d, writes=[idf.b])
        p.dma("sp", idf.t[:, 128:144], gcols_d, writes=[idf.b])
        ident = sb("ident", [128, 128], BF16)
        p.cp("dve", ident.t[:], idf.t[:, 0:128], [idf.b], [ident.b])
        tabs = sb("tabs", [128, T_COLS], F32, dma=True)
        p.dma("sp", tabs.t[:], tabs_d, writes=[tabs.b])
        cosT = sb("cosT", [128, S // 16], F32, dma=True)
        sinT = sb("sinT", [128, S // 16], F32, dma=True)
        p.dma("sp", cosT.t[:], cos_d, writes=[cosT.b])
        p.dma("sp", sinT.t[:], sin_d, writes=[sinT.b])
        W = sb("Wqkv", [128, 3 * 4096], BF16)
        stg = sb("stg", [128, 2048], F32, dma=True)
        for wi, w_d in enumerate((wq_d, wk_d, wv_d)):
            for i in range(2):
                p.dma("sp", stg.t[:], w_d[:, i * 2048:(i + 1) * 2048], writes=[stg.b])
                p.cp("pool", W.t[:, wi * 4096 + i * 2048: wi * 4096 + (i + 1) * 2048], stg.t[:], [stg.b], [W.b])
        xts = [sb("xt", [128, 1024], F32, dma=True) for _ in range(2)]
        junk = sb("junk", [128, 1024], BF16)
        ss = sb("ss", [128, 2], F32)
        rs = sb("rs", [128, 2], F32)
        hn = sb("hn", [128, 1024], BF16)
        hq = sb("hq", [128, 8, 128], BF16)
        hkv = sb("hkv", [128, 8, 128], BF16)
        vsb = [sb("vsb", [128, 512], BF16, dma=True) for _ in range(2)]
        qf = sb("qf", [128, 512], F32)
        sq = sb("sq", [128, 512], F32)
        s8 = sb("s8", [128, 8], F32)
        tmp = [sb("rt", [128, 8, 8], F32) for _ in range(4)]
        qr = sb("qr", [128, 512], BF16)
        qT = [sb("qT", [128, 4, 128], BF16, dma=True) for _ in range(4)]
        ptr = ps("ptr", [128, 1024], BF16)
        pjs = [ps("pj", [128, 512]) for _ in range(3)]
        ptq = [ps("ptq", [128, 512], BF16) for _ in range(2)]
        gq = idf.t[:, 128:136].unsqueeze(2).to_broadcast([128, 8, 128])
        gkv = idf.t[:, 136:144].unsqueeze(2).to_broadcast([128, 8, 128])
        nq = 0
        for i in range(NTL):
            rows = slice(i * 128, (i + 1) * 128)
            xt = xts[i % 2]
            p.dma("sp", xt.t[:], x1_d[rows, :], reads=list(in_bufs), writes=[xt.b])
            emit_rms_T(p, xt, hn, ss, rs, junk, ptr, ident, 1e-6)
            p3 = ptr.t[:, :].rearrange("p (c t) -> p c t", c=8)
            p.tt("dve", hq.t[:], p3, gq, ALU.mult, [ptr.b, idf.b], [hq.b])
            p.tt("dve", hkv.t[:], p3, gkv, ALU.mult, [ptr.b, idf.b], [hkv.b])
            for wi, h_ in ((0, hq), (1, hkv), (2, hkv)):
                for c in range(8):
                    p.mm(pjs[wi].t[:, :], h_.t[:, c, :], W.t[:, wi * 4096 + c * 512: wi * 4096 + (c + 1) * 512],
                         c == 0, c == 7, [h_.b, W.b], [pjs[wi].b])
            v_ = vsb[i % 2]
            p.cp("act", v_.t[:], pjs[2].t[:, :], [pjs[2].b], [v_.b])
            p.dma("pool", V_d[rows, :], v_.t[:], reads=[v_.b], writes=[scr_b])
            for wi, goff, dst_d in ((0, T_QG, QT_d), (1, T_KG, KT_d)):
                a_ = pjs[wi]
                p.cp("act", qf.t[:], a_.t[:, :], [a_.b], [qf.b])
                p.act(sq.t[:], a_.t[:, :], AF.Square, [a_.b], [sq.b])
                p.op("dve", lambda e: e.tensor_reduce(s8.t[:], sq.t[:].rearrange("p (h d) -> p h d", h=8),
                                                      AX.X, ALU.add), [sq.b], [s8.b])
                p.ts("dve", s8.t[:], s8.t[:], 1.0 / 64, 1e-6, ALU.mult, ALU.add, [s8.b], [s8.b])
                p.act(s8.t[:], s8.t[:], AF.Sqrt, [s8.b], [s8.b])
                p.op("dve", lambda e: e.reciprocal(s8.t[:], s8.t[:]), [s8.b], [s8.b])
                q3 = qf.t[:].rearrange("p (h d) -> p h d", h=8)
                p.tt("dve", q3, q3, s8.t[:, :].unsqueeze(2).to_broadcast([128, 8, 64]), ALU.mult, [qf.b, s8.b], [qf.b])
                p.tt("dve", qf.t[:], qf.t[:], tabs.t[:, goff:goff + 512], ALU.mult, [qf.b, tabs.b], [qf.b])
                cB = cosT.t[:, i * 8:(i + 1) * 8].unsqueeze(1).to_broadcast([128, 8, 8])
                sB = sinT.t[:, i * 8:(i + 1) * 8].unsqueeze(1).to_broadcast([128, 8, 8])
                x1v, x2v = q3[:, :, 0:8], q3[:, :, 8:16]
                p.tt("dve", tmp[0].t[:], x1v, cB, ALU.mult, [qf.b, cosT.b], [tmp[0].b])
                p.tt("pool", tmp[1].t[:], x2v, sB, ALU.mult, [qf.b, sinT.b], [tmp[1].b])
                p.tt("dve", tmp[2].t[:], x2v, cB, ALU.mult, [qf.b, cosT.b], [tmp[2].b])
                p.tt("pool", tmp[3].t[:], x1v, sB, ALU.mult, [qf.b, sinT.b], [tmp[3].b])
                p.cp("act", qr.t[:], qf.t[:], [qf.b], [qr.b])
                r3 = qr.t[:].rearrange("p (h d) -> p h d", h=8)
                p.tt("dve", r3[:, :, 0:8], tmp[0].t[:], tmp[1].t[:], ALU.subtract, [tmp[0].b, tmp[1].b], [qr.b])
                p.tt("dve", r3[:, :, 8:16], tmp[2].t[:], tmp[3].t[:], ALU.add, [tmp[2].b, tmp[3].b], [qr.b])
                pq_ = ptq[wi]
                for j in range(4):
                    p.tr(pq_.t[:, j * 128:(j + 1) * 128], qr.t[:, j * 128:(j + 1) * 128], ident.t[:],
                         [qr.b, ident.b], [pq_.b])
                qT_ = qT[nq % 4]
                nq += 1
                p.cp("act", qT_.t[:], pq_.t[:, :].rearrange("p (j t) -> p j t", j=4), [pq_.b], [qT_.b])
                p.dma("pool", dst_d[:, :, rows].rearrange("j p t -> p j t"), qT_.t[:], reads=[qT_.b], writes=[scr_b])

    with p.scope():
        sb, ps = p.sb, p.ps
        tabs = sb("tabs", [128, T_COLS], F32, dma=True)
        p.dma("sp", tabs.t[:], tabs_d, writes=[tabs.b])
        lam = sb("lam", [128, 8], F32)
        lt = sb("lt", [128, 64], F32)
        for k_, (o1, o2) in enumerate(((T_LAM, T_LAM + 64), (T_LAM + 128, T_LAM + 192))):
            p.tt("dve", lt.t[:], tabs.t[:, o1:o1 + 64], tabs.t[:, o2:o2 + 64], ALU.mult, [tabs.b], [lt.b])
            p.op("dve", lambda e: e.tensor_reduce(lam.t[:, k_:k_ + 1], lt.t[:], AX.X, ALU.add), [lt.b], [lam.b])
        p.act(lam.t[:, 0:2], lam.t[:, 0:2], AF.Exp, [lam.b], [lam.b])
        p.tt("dve", lam.t[:, 2:3], lam.t[:, 1:2], lam.t[:, 0:1], ALU.subtract, [lam.b], [lam.b])
        p.ts("dve", lam.t[:, 3:4], lam.t[:, 2:3], -LAM_INIT, None, ALU.add, None, [lam.b], [lam.b])
        neglam = lam.t[:, 3:4]
        sgt = sb("sgt", [128, 128], F32)
        p.ts("dve", sgt.t[:], tabs.t[:, T_SG:T_SG + 128], 1.0 - LAM_INIT, None, ALU.mult, None, [tabs.b], [sgt.b])
        zer = sb("zer", [128, 512], BF16)
        p.op("pool", lambda e: e.memset(zer.t[:], 0.0), [], [zer.b])
        KT = [sb("KT", [128, S], BF16, dma=True) for _ in range(2)]
        QT = [sb("QT", [128, S], BF16, dma=True) for _ in range(2)]
        Vt = [sb("Vt", [128, NTL, 129], BF16, dma=True) for _ in range(2)]
        for v_ in Vt:
            p.op("pool", lambda e: e.memset(v_.t[:, :, 128:129], 1.0), [], [v_.b])
        PT = [[sb("PT", [128, 512], BF16) for _ in range(2)] for _ in range(2)]
        accsb = sb("accsb", [128, 3, 387], F32)
        rsum = sb("rsum", [128, 3, 3], F32)
        t1 = sb("t1", [128, 128], F32)
        t2 = sb("t2", [128, 128], F32)
        ob = [sb("ob", [128, 128], BF16, dma=True) for _ in range(2)]
        junk = sb("junk", [128, 128], BF16)
        ss = sb("ss", [128, 2], F32)
        psc = [[ps("psc", [128, 512]) for _ in range(2)] for _ in range(2)]
        pacc = [ps("pacc", [128, 512]) for _ in range(3)]
        it_ = 0
        no = 0
        for j in range(4):
            K_, Q_, V_ = KT[j % 2], QT[j % 2], Vt[j % 2]
            p.dma("sp", K_.t[:], KT_d[j], reads=[scr_b], writes=[K_.b])
            p.dma("sp", Q_.t[:], QT_d[j], reads=[scr_b], writes=[Q_.b])
            p.dma("sp", V_.t[:, :, 0:128], V_d[:, j * 128:(j + 1) * 128].rearrange("(t p) v -> p t v", p=128),
                  reads=[scr_b], writes=[V_.b])
            for qb in range(S // 512):
                for b_ in range(3):
                    p.mm(pacc[b_].t[:, :], zer.t[:, 0:128], zer.t[:, :], True, True, [zer.b], [pacc[b_].b])
                for kt in range(4 * qb + 4):
                    r = max(kt - 4 * qb, 0)
                    N = 512 - r * 128
                    par = it_ % 2
                    it_ += 1
                    for m in range(2):
                        sl = slice(m * 64, (m + 1) * 64)
                        sc_, pt_ = psc[m][par], PT[m][par]
                        p.mm(sc_.t[:, 0:N], K_.t[sl, kt * 128:(kt + 1) * 128],
                             Q_.t[sl, qb * 512 + r * 128: (qb + 1) * 512], True, True, [K_.b, Q_.b], [sc_.b])
                        p.act(pt_.t[:, 0:N], sc_.t[:, 0:N], AF.Exp, [sc_.b], [pt_.b], scale=0.125)
                        if kt >= 4 * qb:
                            p.op("pool", lambda e: e.memset(pt_.t[64:128, 0:64], 0.0), [], [pt_.b])
                    for m in range(2):
                        pt_ = PT[m][par]
                        for qs in range(r, 4):
                            a = m * 4 + qs
                            col = (qs - r) * 128
                            p.mm(pacc[a // 3].t[:, (a % 3) * 129:(a % 3) * 129 + 129], pt_.t[:, col:col + 128],
                                 V_.t[:, kt, :], False, False, [pt_.b, V_.b], [pacc[a // 3].b])
                for b_ in range(3):
                    p.cp("act", accsb.t[:, b_, :], pacc[b_].t[:, 0:387], [pacc[b_].b], [accsb.b])
                a4 = accsb.t[:].rearrange("p b (s c) -> p b s c", s=3)
                p.op("dve", lambda e: e.reciprocal(rsum.t[:, 0:2, :], a4[:, 0:2, :, 128]), [accsb.b], [rsum.b])
                p.op("dve", lambda e: e.reciprocal(rsum.t[:, 2, 0:2], a4[:, 2, 0:2, 128]), [accsb.b], [rsum.b])
                for qs in range(4):
                    a0, a1 = qs, 4 + qs
                    p.act(t1.t[:], a4[:, a0 // 3, a0 % 3, 0:128], AF.Copy, [accsb.b, rsum.b], [t1.b],
                          scale=rsum.t[:, a0 // 3, a0 % 3: a0 % 3 + 1])
                    p.act(t2.t[:], a4[:, a1 // 3, a1 % 3, 0:128], AF.Copy, [accsb.b, rsum.b], [t2.b],
                          scale=rsum.t[:, a1 // 3, a1 % 3: a1 % 3 + 1])
                    p.stt("dve", t1.t[:], t2.t[:], neglam, t1.t[:], ALU.mult, ALU.add, [t1.b, t2.b, lam.b], [t1.b])
                    p.act(junk.t[:], t1.t[:], AF.Square, [t1.b], [junk.b, ss.b], accum_out=ss.t[:, 0:1])
                    p.ts("dve", ss.t[:, 1:2], ss.t[:, 0:1], 1.0 / 128, 1e-5, ALU.mult, ALU.add, [ss.b], [ss.b])
                    p.act(ss.t[:, 1:2], ss.t[:, 1:2], AF.Sqrt, [ss.b], [ss.b])
                    p.op("dve", lambda e: e.reciprocal(ss.t[:, 1:2], ss.t[:, 1:2]), [ss.b], [ss.b])
                    p.act(t2.t[:], t1.t[:], AF.Copy, [t1.b, ss.b], [t2.b], scale=ss.t[:, 1:2])
                    o_ = ob[no % 2]
                    no += 1
                    p.tt("dve", o_.t[:], t2.t[:], sgt.t[:], ALU.mult, [t2.b, sgt.b], [o_.b])
                    q0 = qb * 512 + qs * 128
                    p.dma("pool", att_d[q0:q0 + 128, j * 128:(j + 1) * 128], o_.t[:], reads=[o_.b], writes=[att_b])


def build_C(S):
    nc = bass.Bass("TRN2", target_bir_lowering=False)
    dt_ = lambda n, sh, dt, kind="ExternalInput": nc.dram_tensor(n, sh, dt, kind=kind).ap()
    x1_d = dt_("x1", [S, D], F32)
    wq_d, wk_d, wv_d = dt_("wq", [128, 4096], F32), dt_("wk", [128, 4096], F32), dt_("wv", [128, 4096], F32)
    gcols_d, tabs_d, id_d = dt_("gcols", [128, 16], F32), dt_("tabs", [128, T_COLS], F32), dt_("ident", [128, 128], F32)
    cos_d, sin_d = dt_("cos", [128, S // 16], F32), dt_("sin", [128, S // 16], F32)
    QT_d = dt_("QT", [4, 128, S], BF16, "Internal")
    KT_d = dt_("KT", [4, 128, S], BF16, "Internal")
    V_d = dt_("V", [S, 512], BF16, "Internal")
    att_d = dt_("att", [S, 512], BF16, "ExternalOutput")
    with contextlib.ExitStack() as st:
        p = P(nc, st)
        att_b = p.dram("att")
        emit_C(p, S, x1_d, wq_d, wk_d, wv_d, gcols_d, tabs_d, id_d, cos_d, sin_d, QT_d, KT_d, V_d, att_d, att_b)
        p.wait_all("pool", [att_b])
        print("phase C instructions:", p.n_ins)
    return nc


_NC_CACHE = {}


def _get(name, builder):
    if name not in _NC_CACHE:
        _NC_CACHE[name] = builder()
    return _NC_CACHE[name]


def _run(nc, maps):
    return run_bass_kernel_spmd(nc, maps, core_ids=list(range(8))).results


def kernel(**inp):
    inp = {k: np.asarray(v) for k, v in inp.items()}
    x = np.ascontiguousarray(inp["x"], dtype=np.float32)
    S, H2 = SEQ, SEQ // 2
    cores = [(c // 2, c % 2) for c in range(8)]
    ncA = _get("A", lambda: build_A(S))
    pa = [prep_A(inp, h) for h in range(2)]
    resA = _run(ncA, [dict(pa[h], x=x[b]) for (b, h) in cores])
    yg = [np.concatenate([np.asarray(resA[2 * b]["yg"]), np.asarray(resA[2 * b + 1]["yg"])], axis=1)
          for b in range(BATCH)]
    ncB = _get("B", lambda: build_B(H2))
    pb = prep_B(inp["rw_w_o"][0], inp["g_ffn"][0], inp["ffn_w_gu"][0], inp["ffn_w_down"][0])
    resB = _run(ncB, [dict(pb, a=np.ascontiguousarray(yg[b][h * H2:(h + 1) * H2]),
                           xres=np.ascontiguousarray(x[b, h * H2:(h + 1) * H2])) for (b, h) in cores])
    x1 = [np.concatenate([np.asarray(resB[2 * b]["x2"]), np.asarray(resB[2 * b + 1]["x2"])], axis=0)
          for b in range(BATCH)]
    ncC = _get("C", lambda: build_C(S))
    pcs = [prep_C(inp, h, S) for h in range(2)]
    resC = _run(ncC, [dict(pcs[h], x1=x1[b]) for (b, h) in cores])
    att = [np.concatenate([np.asarray(resC[2 * b]["att"]), np.asarray(resC[2 * b + 1]["att"])], axis=1)
           for b in range(BATCH)]
    pd = prep_B(inp["da_w_o"][0], inp["g_ffn"][1], inp["ffn_w_gu"][1], inp["ffn_w_down"][1])
    resD = _run(ncB, [dict(pd, a=np.ascontiguousarray(att[b][h * H2:(h + 1) * H2]),
                           xres=np.ascontiguousarray(x1[b][h * H2:(h + 1) * H2])) for (b, h) in cores])
    out = np.stack([np.concatenate([np.asarray(resD[2 * b]["x2"]), np.asarray(resD[2 * b + 1]["x2"])], axis=0)
                    for b in range(BATCH)], axis=0)
    return out.astype(np.float32)
```
